# Optimizing a Trainium2 kernel written in Bass

```python
import jax, jax.numpy as jnp
from jax import lax
import numpy as np

D_MODEL = 4096
BATCH = 4
SEQ = 2048
DEPTH = 2

D_MIX = D_MODEL
W_A = D_MIX // 2
W_B = D_MIX - W_A
H_A = 8
H_B = 8
D_B = W_B // H_B
CONV_K = 31
CHUNK = 128
N_MEM = 256
XA_HEADS = 4
XA_DHEAD = D_MODEL // XA_HEADS
D_FF = 4 * D_MODEL
EPS = 1e-6

kernel_name = "hybrid_conv_gmlp_memxattn_block"


def _rmsnorm(x, g):
    xf = x.astype(jnp.float32)
    y = xf * lax.rsqrt(jnp.mean(xf * xf, axis=-1, keepdims=True) + EPS)
    return (y * g.astype(jnp.float32)).astype(x.dtype)


def _group_layernorm(x, g, b, n_groups):
    shp = x.shape
    xf = x.astype(jnp.float32).reshape(*shp[:-1], n_groups, shp[-1] // n_groups)
    mu = jnp.mean(xf, axis=-1, keepdims=True)
    xc = xf - mu
    var = jnp.mean(xc * xc, axis=-1, keepdims=True)
    y = (xc * lax.rsqrt(var + EPS)).reshape(shp)
    return (y * g.astype(jnp.float32) + b.astype(jnp.float32)).astype(x.dtype)


def _causal_depthwise_conv(x, w, b):
    C = x.shape[-1]
    y = lax.conv_general_dilated(
        x, w[:, None, :].astype(x.dtype), window_strides=(1,),
        padding=[(CONV_K - 1, 0)], dimension_numbers=('NWC', 'WIO', 'NWC'),
        feature_group_count=C)
    return y + b.astype(x.dtype)


def _conformer_conv_group(z_a, conv_w, conv_b, ln_g, ln_b):
    a, gate = jnp.split(z_a, 2, axis=-1)
    h = a * jax.nn.sigmoid(gate)
    h = _causal_depthwise_conv(h, conv_w, conv_b)
    h = _group_layernorm(h, ln_g, ln_b, H_A)
    return jax.nn.silu(h)


def _gmlp_group(z_b, ln_g, ln_b, w_s, b_s):
    z_b = jax.nn.gelu(z_b)
    u, v = jnp.split(z_b, 2, axis=-1)
    v = _group_layernorm(v, ln_g, ln_b, H_B)
    B, S, C = v.shape
    vc = v.reshape(B, S // CHUNK, CHUNK, H_B, D_B)
    causal = jnp.tril(jnp.ones((CHUNK, CHUNK), dtype=bool))
    w = jnp.where(causal[None], w_s, 0.0).astype(v.dtype)
    s = jnp.einsum('hts,bcshd->bcthd', w, vc)
    s = s + b_s.T.astype(v.dtype)[None, None, :, :, None]
    return u * s.reshape(B, S, C)


def _memory_cross_attention(h, m, w_q, w_kv, w_o):
    B, S, _ = h.shape
    q = (h @ w_q).reshape(B, S, XA_HEADS, XA_DHEAD)
    k, v = jnp.split(m @ w_kv, 2, axis=-1)
    k = k.reshape(B, N_MEM, XA_HEADS, XA_DHEAD)
    v = v.reshape(B, N_MEM, XA_HEADS, XA_DHEAD)
    scores = jnp.einsum('bshd,bmhd->bhsm', q, k).astype(jnp.float32) * (XA_DHEAD ** -0.5)
    p = jax.nn.softmax(scores, axis=-1).astype(v.dtype)
    o = jnp.einsum('bhsm,bmhd->bshd', p, v).reshape(B, S, D_MODEL)
    return o @ w_o


def setup_inputs(seed: int = 0) -> dict:
    key = jax.random.key(seed)
    ks = jax.random.split(key, 24)
    f32 = jnp.float32

    def nrm(k, shape, scale):
        return jax.random.normal(k, shape, f32) * scale

    def gain(k, n):
        return 1.0 + nrm(k, (DEPTH, n), 0.02)

    L = DEPTH
    return {
        "x": nrm(ks[0], (BATCH, SEQ, D_MODEL), 1.0),
        "mem": nrm(ks[1], (BATCH, N_MEM, D_MODEL), 1.0),
        "g_pre_mix": gain(ks[2], D_MODEL),
        "w_in": nrm(ks[3], (L, D_MODEL, 2 * W_A + 2 * W_B), D_MODEL ** -0.5),
        "conv_w": nrm(ks[4], (L, CONV_K, W_A), CONV_K ** -0.5),
        "conv_b": nrm(ks[5], (L, W_A), 0.02),
        "ln_a_g": gain(ks[6], W_A),
        "ln_a_b": nrm(ks[7], (L, W_A), 0.02),
        "ln_v_g": gain(ks[8], W_B),
        "ln_v_b": nrm(ks[9], (L, W_B), 0.02),
        "w_spatial": nrm(ks[10], (L, H_B, CHUNK, CHUNK), 0.5 * CHUNK ** -0.5),
        "b_spatial": 1.0 + nrm(ks[11], (L, H_B, CHUNK), 0.02),
        "w_out": nrm(ks[12], (L, D_MIX, D_MODEL), D_MIX ** -0.5),
        "g_post_mix": gain(ks[13], D_MODEL),
        "g_pre_xa": gain(ks[14], D_MODEL),
        "g_mem": gain(ks[15], D_MODEL),
        "w_q": nrm(ks[16], (L, D_MODEL, D_MODEL), D_MODEL ** -0.5),
        "w_kv": nrm(ks[17], (L, D_MODEL, 2 * D_MODEL), D_MODEL ** -0.5),
        "w_o": nrm(ks[18], (L, D_MODEL, D_MODEL), D_MODEL ** -0.5),
        "g_post_xa": gain(ks[19], D_MODEL),
        "g_pre_mlp": gain(ks[20], D_MODEL),
        "w_up": nrm(ks[21], (L, D_MODEL, D_FF), D_MODEL ** -0.5),
        "w_down": nrm(ks[22], (L, D_FF, D_MODEL), D_FF ** -0.5),
        "g_post_mlp": gain(ks[23], D_MODEL),
    }


def reference(x, mem, g_pre_mix, w_in, conv_w, conv_b, ln_a_g, ln_a_b, ln_v_g, ln_v_b,
              w_spatial, b_spatial, w_out, g_post_mix, g_pre_xa, g_mem, w_q, w_kv, w_o,
              g_post_xa, g_pre_mlp, w_up, w_down, g_post_mlp):
    for l in range(DEPTH):
        h = _rmsnorm(x, g_pre_mix[l])
        z = h @ w_in[l]
        y_a = _conformer_conv_group(z[..., :2 * W_A], conv_w[l], conv_b[l],
                                    ln_a_g[l], ln_a_b[l])
        y_b = _gmlp_group(z[..., 2 * W_A:], ln_v_g[l], ln_v_b[l],
                          w_spatial[l], b_spatial[l])
        mix = jnp.concatenate([y_a, y_b], axis=-1) @ w_out[l]
        x = x + _rmsnorm(mix, g_post_mix[l])
        h = _rmsnorm(x, g_pre_xa[l])
        m = _rmsnorm(mem, g_mem[l])
        xa = _memory_cross_attention(h, m, w_q[l], w_kv[l], w_o[l])
        x = x + _rmsnorm(xa, g_post_xa[l])
        h = _rmsnorm(x, g_pre_mlp[l])
        ff = jnp.square(jax.nn.relu(h @ w_up[l])) @ w_down[l]
        x = x + _rmsnorm(ff, g_post_mlp[l])
    return x
```

```python
import numpy as np
from contextlib import ExitStack
import concourse.bass as bass
import concourse.mybir as mybir
from concourse.bass_utils import run_bass_kernel_spmd

F32 = mybir.dt.float32
BF16 = mybir.dt.bfloat16
AF = mybir.ActivationFunctionType
ALU = mybir.AluOpType
AX = mybir.AxisListType

D = 4096
NCH = 32
EPS = 1e-6
CELL = 256
NSLOT = 3
SLAB = 4096
NS_L = 416
NS_KV = 64
HALO = 32
GELU_TANH = True
USE_POW = False

SB_BASE = 16640
X_OFF = SB_BASE
ACC_OFF = SB_BASE + 65536
H_OFF = SB_BASE + 131072
W_OFF = SB_BASE + 163840
HID_OFF = W_OFF + NSLOT * 8192
MISC_OFF = HID_OFF + 8192
C_OFF = MISC_OFF
PP_OFF = C_OFF + 768
SQ_OFF = PP_OFF + 3328
HS_OFF = SQ_OFF + 2048
XH_OFF = HS_OFF + 2048
HH_OFF = XH_OFF + 4096
DG_OFF = HH_OFF + 2048
SM_OFF = DG_OFF + 1024

PC_GPRE_MIX, PC_GPOST_MIX, PC_GPRE_XA, PC_GPOST_XA, PC_GPRE_MLP, PC_GPOST_MLP, PC_GMEM = 0, 32, 64, 96, 128, 160, 192
PC_CONVW, PC_CONVB, PC_LNAG, PC_LNAB, PC_LNVG, PC_LNVB = 224, 720, 736, 752, 768, 784
NPP = 800


class V:
    __slots__ = ("ap", "keys", "excl")

    def __init__(self, ap, keys, excl=False):
        self.ap = ap
        self.keys = keys
        self.excl = excl


def _cells(space, lo, hi):
    return [(space, i) for i in range(lo // CELL, (hi - 1) // CELL + 1)]


class MT:
    def __init__(self, h, fshape, es, space, off):
        self.h = h
        self.fshape = list(fshape)
        self.es = es
        self.space = space
        self.off = off
        st = [1] * len(fshape)
        for i in range(len(fshape) - 2, -1, -1):
            st[i] = st[i + 1] * fshape[i + 1]
        self.st = st

    def __call__(self, *idx, p=None):
        key = [slice(None) if p is None else slice(p[0], p[1])]
        lo = 0
        hi = 0
        for d, n in enumerate(self.fshape):
            i = idx[d] if d < len(idx) else None
            if i is None:
                a, b = 0, n
                key.append(slice(None))
            elif isinstance(i, int):
                a, b = i, i + 1
                key.append(i)
            else:
                a, b = i
                key.append(slice(a, b))
            lo += a * self.st[d]
            hi += (b - 1) * self.st[d]
        hi += 1
        if self.space.startswith("ps"):
            return V(self.h[tuple(key)], [(self.space, 0)], True)
        return V(self.h[tuple(key)], _cells(self.space, self.off + lo * self.es, self.off + hi * self.es))


class Prog:
    def __init__(self):
        self.recs = []
        self.cells = {}
        self.last_dma = {}

    def add(self, eng, fn, reads=(), writes=(), dma=None):
        i = len(self.recs)
        raw = set()
        oth = set()
        cells = self.cells
        for v in reads:
            for k in v.keys:
                c = cells.get(k)
                if c is not None and c[0] is not None:
                    raw.add(c[0])
                if c is not None and v.excl:
                    for rk2, j2 in c[1].items():
                        if rk2 != eng:
                            oth.add(j2)
        for v in writes:
            for k in v.keys:
                c = cells.get(k)
                if c is not None:
                    if c[0] is not None:
                        oth.add(c[0])
                    oth.update(c[1].values())
        if dma is not None and dma in self.last_dma:
            oth.add(self.last_dma[dma])
        rk = eng if dma is None else "dma:" + dma
        for v in reads:
            for k in v.keys:
                c = cells.get(k)
                if c is None:
                    c = [None, {}]
                    cells[k] = c
                c[1][rk] = i
        for v in writes:
            for k in v.keys:
                cells[k] = [i, {}]
        if dma is not None:
            self.last_dma[dma] = i
        self.recs.append((eng, fn, raw, oth - raw, dma))
        return i

    def finalize(self):
        recs = self.recs
        n = len(recs)
        red = [None] * n
        need = set()
        for i, (eng, fn, raw, oth, dma) in enumerate(recs):
            best = {}
            for grp, israw in ((raw, True), (oth, False)):
                for j in grp:
                    je, _, _, _, jd = recs[j]
                    if jd is None and dma is None and je == eng:
                        if eng == "pe" or not israw:
                            continue
                    key = ("E", je) if jd is None else ("D", jd)
                    if key not in best or best[key] < j:
                        best[key] = j
            red[i] = best
            need.update(best.values())
        sig = [None] * n
        cnt = {}
        for i, (eng, fn, raw, oth, dma) in enumerate(recs):
            if dma is not None:
                k = ("D", dma)
                cnt[k] = cnt.get(k, 0) + 16
                sig[i] = (k, cnt[k], 16)
            elif i in need:
                k = ("E", eng)
                cnt[k] = cnt.get(k, 0) + 1
                sig[i] = (k, cnt[k], 1)
        self.red = red
        self.sig = sig
        self.semkeys = sorted(cnt.keys())
        self.final = dict(cnt)

    def emit(self, nc, es, final_wait_keys):
        sems = {}
        for k in self.semkeys:
            sems[k] = es.enter_context(nc.semaphore("s_%s_%s" % (k[0], k[1].replace(":", "_"))))
        block = es.enter_context(nc.Block())
        recs, red, sig = self.recs, self.red, self.sig
        per = {"pe": [], "act": [], "dve": [], "pool": [], "sp": []}
        for i, r in enumerate(recs):
            per[r[0]].append(i)

        def run(engname, e):
            waited = {}
            for i in per[engname]:
                for k, j in red[i].items():
                    sk, val, _ = sig[j]
                    if waited.get(sk, 0) < val:
                        e.wait_ge(sems[sk], val)
                        waited[sk] = val
                ins = recs[i][1](e)
                s = sig[i]
                if s is not None:
                    ins.then_inc(sems[s[0]], s[2])
            if engname == "sp":
                for k in final_wait_keys:
                    if k in self.final:
                        e.wait_ge(sems[k], self.final[k])

        @block.tensor
        def _(e):
            run("pe", e)

        @block.scalar
        def _(e):
            run("act", e)

        @block.vector
        def _(e):
            run("dve", e)

        @block.gpsimd
        def _(e):
            run("pool", e)

        @block.sync
        def _(e):
            run("sp", e)


class K:
    pass


def build(mode):
    fused = mode == "fused"
    NL = 2 if fused else 1
    NTOK = 1152 if fused else 1024
    nc = bass.Bass("TRN2", target_bir_lowering=False)
    P = Prog()
    es = ExitStack()

    def din(name, shape, dt=F32):
        return nc.dram_tensor(name, list(shape), dt, kind="ExternalInput").ap()

    xT = din("xT", [D, NTOK])
    xh_d = None if fused else din("xh", [D, HALO])
    memT = din("memT", [D, 256])
    wl_d = din("wl", [NL * NS_L, 128, SLAB])
    wkv_d = din("wkv", [NL * NS_KV, 128, SLAB])
    pp_d = din("pp", [NL * 128, NPP])
    wsp_d = din("wsp", [NL * 128, 1024])
    bsb_d = din("bsb", [NL * 128, 1024])
    cst_d = din("cst", [128, 384])
    flag_d = din("flag", [128, 1])
    outT = nc.dram_tensor("outT", [D, 1024], F32, kind="ExternalOutput").ap()
    Kd = [nc.dram_tensor("Kd%d" % l, [128, 32, 256], BF16).ap() for l in range(NL)]
    Vd = [nc.dram_tensor("Vd%d" % l, [128, 2, 4096], BF16).ap() for l in range(NL)]

    tcache = {}

    def sb(name, fshape, dt, off):
        key = (name, tuple(fshape), off)
        if key not in tcache:
            h = nc.alloc_sbuf_tensor_at("%s_%d" % (name, len(tcache)), [128] + list(fshape), dt, offset=off)
            tcache[key] = MT(h, fshape, 2 if dt == BF16 else 4, "sb", off)
        return tcache[key]

    PSB = []
    for b in range(7):
        h = es.enter_context(nc.psum_tensor("psb%d" % b, [128, 512], F32))
        PSB.append(MT(h, [512], 4, "ps%d" % b, 0))
    h7 = es.enter_context(nc.psum_tensor("psb7", [128, 1024], BF16))
    PS7 = MT(h7, [1024], 2, "ps7", 0)
    rot = [0]

    def nextbank():
        b = PSB[rot[0] % 4]
        rot[0] += 1
        return b

    CST = sb("cst", [3, 128], BF16, C_OFF)
    ident = CST(0)
    ones = CST(1)
    tril = CST(2)
    PPt = sb("pp", [NPP], F32, PP_OFF)
    SQt = sb("sqt", [2, 512], BF16, SQ_OFF)
    RLt = sb("rlt", [512], F32, SQ_OFF)
    HSt = sb("hsave", [2, 16, HALO], BF16, HS_OFF)
    XH1 = sb("xh1", [32, HALO], F32, XH_OFF)
    HHt = sb("hhalo", [32, HALO], BF16, HH_OFF)
    DGt = sb("diag", [16, 128], BF16, HID_OFF)
    SMt = sb("small", [8], F32, SM_OFF)
    WSL = [sb("wslot%d" % s, [SLAB], BF16, W_OFF + s * 8192) for s in range(NSLOT)]

    def dkey(name):
        return [V(None, [("dr", name)])]

    def mm(out, lhsT, rhs, start, stop):
        P.add("pe", lambda e, o=out.ap, l=lhsT.ap, r=rhs.ap, s=start, t=stop: e.matmul(o, l, r, start=s, stop=t),
              reads=[lhsT, rhs], writes=[out])

    def tr(out, in_, idv):
        P.add("pe", lambda e, o=out.ap, i=in_.ap, d=idv.ap: e.transpose(o, i, d), reads=[in_, idv], writes=[out])

    def act(out, in_, func, bias=None, scale=None, accum=None, eng="act"):
        rd = [in_]
        kw = {}
        if bias is not None:
            if isinstance(bias, V):
                rd.append(bias)
                kw["bias"] = bias.ap
            else:
                kw["bias"] = bias
        if scale is not None:
            if isinstance(scale, V):
                rd.append(scale)
                kw["scale"] = scale.ap
            else:
                kw["scale"] = scale
        wr = [out]
        if accum is not None:
            wr.append(accum)
            kw["accum_out"] = accum.ap
        P.add("act", lambda e, o=out.ap, i=in_.ap, f=func, kw=kw: e.activation(out=o, in_=i, func=f, **kw),
              reads=rd, writes=wr)

    def stt(out, in0, scalar, in1, op0, op1, eng="dve"):
        rd = [in0, in1]
        if isinstance(scalar, V):
            rd.append(scalar)
            sc = scalar.ap
        else:
            sc = scalar
        P.add(eng, lambda e, o=out.ap, a=in0.ap, s=sc, b=in1.ap, p0=op0, p1=op1:
              e.scalar_tensor_tensor(out=o, in0=a, scalar=s, in1=b, op0=p0, op1=p1), reads=rd, writes=[out])

    def tt(out, in0, in1, op, eng="dve"):
        P.add(eng, lambda e, o=out.ap, a=in0.ap, b=in1.ap, p=op: e.tensor_tensor(out=o, in0=a, in1=b, op=p),
              reads=[in0, in1], writes=[out])

    def ts(out, in0, s1, s2, op0, op1=None, eng="dve"):
        rd = [in0]
        a1 = s1
        a2 = s2
        if isinstance(s1, V):
            rd.append(s1)
            a1 = s1.ap
        if isinstance(s2, V):
            rd.append(s2)
            a2 = s2.ap
        if op1 is None:
            P.add(eng, lambda e, o=out.ap, a=in0.ap, x=a1, p0=op0:
                  e.tensor_scalar(out=o, in0=a, scalar1=x, scalar2=None, op0=p0), reads=rd, writes=[out])
        else:
            P.add(eng, lambda e, o=out.ap, a=in0.ap, x=a1, y=a2, p0=op0, p1=op1:
                  e.tensor_scalar(out=o, in0=a, scalar1=x, scalar2=y, op0=p0, op1=p1), reads=rd, writes=[out])

    def cp(out, in_, eng="dve"):
        P.add(eng, lambda e, o=out.ap, i=in_.ap: e.tensor_copy(out=o, in_=i), reads=[in_], writes=[out])

    def dma(q, key, out_ap, in_ap, reads, writes):
        P.add(q, lambda e, o=out_ap, i=in_ap: e.dma_start(out=o, in_=i), reads=reads, writes=writes, dma=key)

    def rstd_inplace(v, inv_n):
        ts(v, v, inv_n, EPS, ALU.mult, ALU.add)
        if USE_POW:
            ts(v, v, -0.5, None, ALU.pow)
        else:
            act(v, v, AF.Ln)
            act(v, v, AF.Exp, scale=-0.5)

    def ppc(col, n=1):
        return PPt((col, col + n))

    ESL = [sb("eslot%d" % s_, [SLAB], BF16, X_OFF + 32768 + s_ * 8192) for s_ in range(4)]
    ESL += [sb("eslot%d" % (4 + s_), [SLAB], BF16, H_OFF + 16384 + s_ * 8192) for s_ in range(2)]
    rings = {0: WSL, 1: ESL}
    ring_members = {0: [], 1: []}
    wseq = []
    ord_in_ring = []
    wstate = {"issued": 0, "used": 0}
    MAXLA = 9

    def plan_slabs(lst, ring=0):
        for ap_ in lst:
            k = len(wseq)
            wseq.append((ap_, ring))
            ord_in_ring.append(len(ring_members[ring]))
            ring_members[ring].append(k)

    def slot_of(k):
        r = wseq[k][1]
        return rings[r][ord_in_ring[k] % len(rings[r])], "w%d_%d" % (r, ord_in_ring[k] % len(rings[r]))

    def can_issue(k, i):
        r = wseq[k][1]
        n = ord_in_ring[k]
        R = len(rings[r])
        return n < R or ring_members[r][n - R] < i

    def pump(i):
        while wstate["issued"] < len(wseq) and wstate["issued"] < i + MAXLA and can_issue(wstate["issued"], i):
            k = wstate["issued"]
            slot, key = slot_of(k)
            v = slot()
            dma("pool", key, v.ap, wseq[k][0], [], [v])
            wstate["issued"] += 1

    def next_slab():
        i = wstate["used"]
        pump(i)
        assert wstate["issued"] > i
        wstate["used"] += 1
        return slot_of(i)[0]

    def prenorm(T, Xv, gcol, Hv, nch=NCH):
        stat = PSB[4]((0, T))
        for c in range(nch):
            sq = SQt(c % 2, (0, T))
            act(sq, Xv(c), AF.Square)
            mm(stat, ones, sq, c == 0, c == nch - 1)
        rstd_inplace(stat, 1.0 / D)
        for c in range(nch):
            stt(Hv(c), Xv(c), ppc(gcol + c), stat, ALU.mult, ALU.mult)

    def postnorm_residual(T, ACCv, gcol, Xv, stat):
        rstd_inplace(stat, 1.0 / D)
        for c in range(NCH):
            a = ACCv(c)
            stt(a, a, ppc(gcol + c), stat, ALU.mult, ALU.mult)
            tt(Xv(c), Xv(c), a, ALU.add)

    def group_ln(T, srcs, tmpC, tmpB, gcol, bcol, func, dsts, post=None):
        s1 = PSB[4]((0, T))
        s2 = PSB[5]((0, T))
        for i in range(2):
            cp(tmpC(i), srcs[i])
            act(SQt(i, (0, T)), srcs[i], AF.Square)
        yield
        for i in range(2):
            mm(s1, ones, tmpC(i), i == 0, i == 1)
        for i in range(2):
            mm(s2, ones, SQt(i, (0, T)), i == 0, i == 1)
        ts(s1, s1, 1.0 / 256.0, None, ALU.mult)
        act(tmpB, s1, AF.Square)
        stt(s2, s2, 1.0 / 256.0, tmpB, ALU.mult, ALU.subtract)
        ts(s2, s2, EPS, None, ALU.add)
        if USE_POW:
            ts(s2, s2, -0.5, None, ALU.pow)
        else:
            act(s2, s2, AF.Ln)
            act(s2, s2, AF.Exp, scale=-0.5)
        yield
        for i in range(2):
            tt(srcs[i], srcs[i], s1, ALU.subtract)
            tt(srcs[i], srcs[i], s2, ALU.mult)
            act(dsts[i], srcs[i], func, bias=ppc(bcol + i), scale=ppc(gcol + i))
        yield
        if post is not None:
            post()
        yield

    def step(g):
        if g is not None:
            next(g, None)

    def gelu_from_psum(dst, src, tmp):
        if GELU_TANH:
            act(dst, src, AF.Gelu_apprx_tanh)
        else:
            act(tmp, src, AF.Square)
            ts(tmp, tmp, 0.044715, 1.0, ALU.mult, ALU.add)
            tt(tmp, tmp, src, ALU.mult)
            act(tmp, tmp, AF.Sigmoid, scale=1.5957691216057308)
            tt(dst, tmp, src, ALU.mult)

    def load_layer_params(l):
        v = PPt()
        dma("sp", "pp", v.ap, pp_d[l * 128:(l + 1) * 128, :], [], [v])

    def views(T):
        o = K()
        o.X = sb("X", [NCH, T], F32, X_OFF)
        o.ACC = sb("ACC", [NCH, T], F32, ACC_OFF)
        o.H = sb("H", [NCH, T], BF16, H_OFF)
        a = ACC_OFF
        o.hh = sb("hh", [16, T + HALO], BF16, a)
        a += 16 * (512 + HALO) * 2
        o.u = sb("u", [16, T], BF16, a)
        a += 16 * 512 * 2
        o.vnT = sb("vnT", [T // 128, 2048], BF16, a)
        a += 4 * 2048 * 2
        o.bsb = sb("bsbt", [8, 128], F32, a)
        a += 4096
        o.Wt = sb("Wt", [8, 128], BF16, a)
        a += 2048
        o.tmpA = sb("tmpA", [2, T], F32, a)
        o.tmpA2 = sb("tmpA2", [2, T], F32, HID_OFF + 4096)
        o.xht = sb("xht", [32, HALO], F32, a)
        a += 4096
        o.tmpB = sb("tmpB", [T], F32, a)
        o.sgh = sb("sgh", [HALO], F32, a)
        a += 2048
        o.tmpC = sb("tmpC", [2, T], BF16, a)
        a += 2048
        assert a <= ACC_OFF + 65536
        o.q = sb("q", [NCH, T], BF16, ACC_OFF)
        o.Kt = sb("Kt", [NCH, 256], BF16, ACC_OFF + 32768)
        o.Vt = sb("Vt", [2, 4096], BF16, ACC_OFF + 49152)
        o.hid = sb("hid", [8, T], BF16, HID_OFF)
        o.Ex = sb("Ex", [256], F32, HID_OFF)
        o.Pb = sb("Pb", [256], BF16, HID_OFF + 1024)
        o.PTs = sb("PTs", [2, 128], BF16, HID_OFF + 1536)
        return o

    def mix(T, o, l, halo_mode, xh_src=None, save_halo=True):
        X, H, ACC = o.X, o.H, o.ACC
        NB = T // 128
        v = o.bsb()
        dma("sp", "bsb", v.ap, bsb_d[l * 128:(l + 1) * 128, :], [], [v])
        v = o.Wt()
        dma("pool", "wsp", v.ap, wsp_d[l * 128:(l + 1) * 128, :], [], [v])
        for h in range(8):
            tt(o.Wt(h), o.Wt(h), tril, ALU.mult)
        prenorm(T, X, PC_GPRE_MIX, H)
        if halo_mode == "x":
            stat = PSB[6]((0, HALO))
            for c in range(NCH):
                sq = SQt(c % 2, (0, HALO))
                act(sq, xh_src(c), AF.Square)
                mm(stat, ones, sq, c == 0, c == NCH - 1)
            rstd_inplace(stat, 1.0 / D)
            for c in range(NCH):
                stt(HHt(c), xh_src(c), ppc(PC_GPRE_MIX + c), stat, ALU.mult, ALU.mult)
        for j in range(16):
            w = next_slab()
            ba = nextbank()
            bha = PSB[6]((64, 64 + HALO))
            bhg = PSB[6]((128, 128 + HALO))
            for kc in range(NCH):
                mm(ba((0, T)), w((kc * 128, kc * 128 + 128)), H(kc), kc == 0, kc == NCH - 1)
                if halo_mode == "x":
                    mm(bha, w((kc * 128, kc * 128 + 128)), HHt(kc), kc == 0, kc == NCH - 1)
            w = next_slab()
            bg = nextbank()
            for kc in range(NCH):
                mm(bg((0, T)), w((kc * 128, kc * 128 + 128)), H(kc), kc == 0, kc == NCH - 1)
                if halo_mode == "x":
                    mm(bhg, w((kc * 128, kc * 128 + 128)), HHt(kc), kc == 0, kc == NCH - 1)
            act(o.tmpB(), bg((0, T)), AF.Sigmoid)
            tt(o.hh(j, (HALO, HALO + T)), ba((0, T)), o.tmpB(), ALU.mult)
            if halo_mode == "x":
                act(o.sgh(), bhg, AF.Sigmoid)
                tt(o.hh(j, (0, HALO)), bha, o.sgh(), ALU.mult)
            elif halo_mode == "saved":
                cp(o.hh(j, (0, HALO)), HSt(l, j))
            else:
                P.add("dve", lambda e, a=o.hh(j, (0, HALO)).ap: e.memset(a, 0.0), reads=[], writes=[o.hh(j, (0, HALO))])
            if save_halo:
                cp(HSt(l, j), o.hh(j, (T, T + HALO)))
        pending = None
        for h in range(8):
            tA = o.tmpA2 if h % 2 else o.tmpA
            for kind, i in (("u", 0), ("u", 1), ("v", 0), ("v", 1)):
                step(pending)
                w = next_slab()
                b = nextbank()
                for kc in range(NCH):
                    mm(b((0, T)), w((kc * 128, kc * 128 + 128)), H(kc), kc == 0, kc == NCH - 1)
                if kind == "u":
                    gelu_from_psum(o.u(2 * h + i), b((0, T)), o.tmpB())
                else:
                    gelu_from_psum(tA(i), b((0, T)), o.tmpB())

            def post(h=h):
                for i in range(2):
                    for tb in range(NB):
                        pt = PS7(((i * NB + tb) * 128, (i * NB + tb) * 128 + 128))
                        tr(pt, o.tmpC(i, (tb * 128, tb * 128 + 128)), ident)
                        cp(o.vnT(tb, (h * 256 + i * 128, h * 256 + i * 128 + 128)), pt)

            pending = group_ln(T, [tA(0), tA(1)], o.tmpC, o.tmpB(), PC_LNVG + 2 * h, PC_LNVB + 2 * h, AF.Identity,
                               [o.tmpC(0), o.tmpC(1)], post)
        dg = [0]
        for j in range(8):
            tA = o.tmpA2 if j % 2 else o.tmpA
            for i in range(2):
                step(pending)
                c = 2 * j + i
                b = nextbank()
                for k in range(31):
                    d = DGt(dg[0] % 16)
                    dg[0] += 1
                    if k % 2 == 0:
                        ts(d, ident, ppc(PC_CONVW + c * 31 + k), None, ALU.mult)
                    else:
                        act(d, ident, AF.Identity, scale=ppc(PC_CONVW + c * 31 + k))
                    mm(b((0, T)), d, o.hh(c, (k + 2, k + 2 + T)), k == 0, k == 30)
                act(tA(i), b((0, T)), AF.Identity, bias=ppc(PC_CONVB + c))
            step(pending)
            step(pending)
            pending = group_ln(T, [tA(0), tA(1)], o.tmpC, o.tmpB(), PC_LNAG + 2 * j, PC_LNAB + 2 * j, AF.Silu,
                               [H(2 * j), H(2 * j + 1)])
        for _ in range(4):
            step(pending)
        for h in range(8):
            for i in range(2):
                b = nextbank()
                for tb in range(NB):
                    mm(b((tb * 128, tb * 128 + 128)), o.vnT(tb, (h * 256 + i * 128, h * 256 + i * 128 + 128)), o.Wt(h),
                       True, True)
                for tb in range(NB):
                    tt(o.tmpB((tb * 128, tb * 128 + 128)), b((tb * 128, tb * 128 + 128)), o.bsb(h), ALU.add)
                tt(H(16 + 2 * h + i), o.tmpB(), o.u(2 * h + i), ALU.mult)
        proj_out(T, o, PC_GPOST_MIX)

    def proj_out(T, o, gcol):
        stat = PSB[5]((0, T))
        pend = None
        for n in range(NCH):
            w = next_slab()
            b = nextbank()
            for kc in range(NCH):
                mm(b((0, T)), w((kc * 128, kc * 128 + 128)), o.H(kc), kc == 0, kc == NCH - 1)
            if pend is not None:
                mm(stat, ones, pend[0], pend[1] == 0, False)
            act(o.ACC(n), b((0, T)), AF.Copy)
            sq = SQt(n % 2, (0, T))
            act(sq, b((0, T)), AF.Square)
            pend = (sq, n)
        mm(stat, ones, pend[0], False, True)
        postnorm_residual(T, o.ACC, gcol, o.X, stat)

    def xattn(T, o, l):
        X, H = o.X, o.H
        NB = T // 128
        prenorm(T, X, PC_GPRE_XA, H)
        v = o.Kt()
        dma("sp", "kld", v.ap, Kd[l], dkey("K%d" % l), [v])
        v = o.Vt()
        dma("sp", "vld", v.ap, Vd[l], dkey("V%d" % l), [v])
        for n in range(NCH):
            w = next_slab()
            b = nextbank()
            for kc in range(NCH):
                mm(b((0, T)), w((kc * 128, kc * 128 + 128)), H(kc), kc == 0, kc == NCH - 1)
            act(o.q(n), b((0, T)), AF.Copy)
        pairs = [(tb, hd) for tb in range(NB) for hd in range(4)]
        NP = len(pairs)
        Exb = [sb("Exb%d" % d_, [256], F32, HID_OFF + d_ * 1024) for d_ in range(2)]
        Pbb = [sb("Pbb%d" % d_, [256], BF16, HID_OFF + 2048 + d_ * 512) for d_ in range(2)]
        PTb = [sb("PTb%d" % d_, [2, 128], BF16, HID_OFF + 3072 + d_ * 512) for d_ in range(2)]
        STb = [[sb("stb%d_%d" % (d_, k_), [1], F32, HID_OFF + 4096 + (d_ * 4 + k_) * 256) for k_ in range(4)]
               for d_ in range(2)]

        def scores(p):
            tb, hd = pairs[p]
            S = nextbank()((0, 256))
            for c in range(8):
                mm(S, o.q(hd * 8 + c, (tb * 128, tb * 128 + 128)), o.Kt(hd * 8 + c), c == 0, c == 7)
            return S

        def softmax(p, S):
            d_ = p % 2
            mx, nmx, sm, rsm = [t_() for t_ in STb[d_]]
            P.add("dve", lambda e, o_=mx.ap, i_=S.ap: e.reduce_max(out=o_, in_=i_, axis=AX.X), reads=[S], writes=[mx])
            ts(nmx, mx, -1.0 / 32.0, None, ALU.mult)
            act(Exb[d_](), S, AF.Exp, bias=nmx, scale=1.0 / 32.0, accum=sm)
            P.add("dve", lambda e, o_=rsm.ap, i_=sm.ap: e.reciprocal(out=o_, in_=i_), reads=[sm], writes=[rsm])
            ts(Pbb[d_](), Exb[d_](), rsm, None, ALU.mult)

        def transposes(p):
            d_ = p % 2
            for mb in range(2):
                pt = PS7((d_ * 256 + mb * 128, d_ * 256 + mb * 128 + 128))
                tr(pt, Pbb[d_]((mb * 128, mb * 128 + 128)), ident)
                cp(PTb[d_](mb), pt)

        def pv(p):
            tb, hd = pairs[p]
            d_ = p % 2
            for half in range(2):
                bo = nextbank()
                for cc in range(4):
                    c = half * 4 + cc
                    for mb in range(2):
                        mm(bo((cc * 128, cc * 128 + 128)),
                           o.Vt(mb, (hd * 1024 + c * 128, hd * 1024 + c * 128 + 128)), PTb[d_](mb), mb == 0, mb == 1)
                for cc in range(4):
                    c = half * 4 + cc
                    act(H(hd * 8 + c, (tb * 128, tb * 128 + 128)), bo((cc * 128, cc * 128 + 128)), AF.Copy)

        Sv = {}
        for p in range(min(2, NP)):
            Sv[p] = scores(p)
            softmax(p, Sv[p])
        for p in range(NP):
            if p + 2 < NP:
                Sv[p + 2] = scores(p + 2)
            transposes(p)
            if p + 2 < NP:
                softmax(p + 2, Sv[p + 2])
            pv(p)
        proj_out(T, o, PC_GPOST_XA)

    def mlp(T, o):
        X, H, ACC = o.X, o.H, o.ACC
        prenorm(T, X, PC_GPRE_MLP, H)
        for g in range(16):
            for j in range(8):
                w = next_slab()
                b = nextbank()
                for kc in range(NCH):
                    mm(b((0, T)), w((kc * 128, kc * 128 + 128)), H(kc), kc == 0, kc == NCH - 1)
                act(RLt((0, T)), b((0, T)), AF.Relu)
                tt(o.hid(j), RLt((0, T)), b((0, T)), ALU.mult)
            for qd in range(8):
                w = next_slab()
                for r in range(4):
                    n = qd * 4 + r
                    b = nextbank()
                    for kc in range(8):
                        mm(b((0, T)), w((kc * 512 + r * 128, kc * 512 + r * 128 + 128)), o.hid(kc), kc == 0, kc == 7)
                    if g == 0:
                        act(ACC(n), b((0, T)), AF.Copy)
                    else:
                        tt(ACC(n), ACC(n), b((0, T)), ALU.add)
        stat = PSB[5]((0, T))
        for n in range(NCH):
            sq = SQt(n % 2, (0, T))
            act(sq, ACC(n), AF.Square)
            mm(stat, ones, sq, n == 0, n == NCH - 1)
        postnorm_residual(T, ACC, PC_GPOST_MLP, X, stat)

    def kv_phase(l):
        T = 256
        Xm = sb("Xm", [NCH, T], F32, X_OFF)
        Hm = sb("Hm", [NCH, T], BF16, H_OFF)
        Kst = sb("Kst", [NCH, 256], BF16, ACC_OFF)
        Vst = sb("Vst", [2, 4096], BF16, ACC_OFF + 16384)
        vtmp = sb("vtmp", [256], BF16, HID_OFF)
        for q4 in range(4):
            v = Xm((q4 * 8, q4 * 8 + 8))
            src = memT[q4 * 1024:(q4 + 1) * 1024, :].rearrange("(c p) t -> p c t", p=128)
            dma("sp", "xin%d" % q4, v.ap, src, [], [v])
        prenorm(T, Xm, PC_GMEM, Hm)
        for n in range(64):
            w = next_slab()
            b = nextbank()
            for kc in range(NCH):
                mm(b((0, T)), w((kc * 128, kc * 128 + 128)), Hm(kc), kc == 0, kc == NCH - 1)
            if n < 32:
                act(Kst(n), b((0, T)), AF.Copy)
            else:
                act(vtmp(), b((0, T)), AF.Copy)
                for mb in range(2):
                    pt = PS7((mb * 128, mb * 128 + 128))
                    tr(pt, vtmp((mb * 128, mb * 128 + 128)), ident)
                    cp(Vst(mb, ((n - 32) * 128, (n - 32) * 128 + 128)), pt)
        v = Kst()
        dma("sp", "kst", Kd[l], v.ap, [v], dkey("K%d" % l))
        v = Vst()
        dma("sp", "vst", Vd[l], v.ap, [v], dkey("V%d" % l))

    def load_x(T, o, col0):
        for q4 in range(4):
            v = o.X((q4 * 8, q4 * 8 + 8))
            src = xT[q4 * 1024:(q4 + 1) * 1024, col0:col0 + T].rearrange("(c p) t -> p c t", p=128)
            dma("sp", "xin%d" % q4, v.ap, src, [], [v])

    def store_x(T, o, col0):
        for q4 in range(4):
            v = o.X((q4 * 8, q4 * 8 + 8))
            dst = outT[q4 * 1024:(q4 + 1) * 1024, col0:col0 + T].rearrange("(c p) t -> p c t", p=128)
            dma("sp", "xout%d" % q4, dst, v.ap, [v], dkey("out%d_%d" % (col0, q4)))

    def layer_slabs(l):
        return [wl_d[l * NS_L + s] for s in range(NS_L)]

    def kv_slabs(l):
        return [wkv_d[l * NS_KV + s] for s in range(NS_KV)]

    if fused:
        plan_slabs(kv_slabs(0) + kv_slabs(1) + layer_slabs(0), 1)
        plan_slabs(layer_slabs(0) + layer_slabs(1) + layer_slabs(0) + layer_slabs(1), 0)
    else:
        plan_slabs(kv_slabs(0), 1)
        plan_slabs(layer_slabs(0) + layer_slabs(0), 0)

    v = CST()
    dma("pool", "cst", v.ap, cst_d, [], [v])
    flg = SMt((4, 5))
    dma("sp", "flag", flg.ap, flag_d, [], [flg])

    def layer(T, o, l, halo_mode, xh_src=None, save_halo=True):
        load_layer_params(l)
        mix(T, o, l, halo_mode, xh_src, save_halo)
        xattn(T, o, l)
        mlp(T, o)

    o512 = views(512)
    if fused:
        load_layer_params(0)
        kv_phase(0)
        load_layer_params(1)
        kv_phase(1)
        oE = views(128)
        load_x(128, oE, 0)
        layer(128, oE, 0, "zero")
        for c in range(NCH):
            ts(XH1(c), oE.X(c, (128 - HALO, 128)), flg, None, ALU.mult)
        load_x(512, o512, 128)
        layer(512, o512, 0, "saved")
        layer(512, o512, 1, "x", xh_src=XH1)
        store_x(512, o512, 0)
        load_x(512, o512, 640)
        layer(512, o512, 0, "saved")
        layer(512, o512, 1, "saved")
        store_x(512, o512, 512)
    else:
        load_layer_params(0)
        kv_phase(0)
        load_x(512, o512, 0)
        for q4 in range(4):
            v = o512.xht((q4 * 8, q4 * 8 + 8))
            src = xh_d[q4 * 1024:(q4 + 1) * 1024, :].rearrange("(c p) t -> p c t", p=128)
            dma("sp", "xh%d" % q4, v.ap, src, [], [v])
        layer(512, o512, 0, "x", xh_src=o512.xht)
        store_x(512, o512, 0)
        load_x(512, o512, 512)
        layer(512, o512, 0, "saved")
        store_x(512, o512, 512)

    assert wstate["used"] == len(wseq), (wstate, len(wseq))
    P.finalize()
    outkeys = [k for k in P.semkeys if k[0] == "D" and k[1].startswith("xout")]
    with es:
        P.emit(nc, es, outkeys)
    return nc


def _std_slabs(W):
    Kd_, N = W.shape
    Wr = W.reshape(Kd_ // 128, 128, N // 128, 128)
    return np.ascontiguousarray(Wr.transpose(2, 1, 0, 3)).reshape(N // 128, 128, (Kd_ // 128) * 128)


def _down_slabs(Wd, g):
    blk = Wd[g * 1024:(g + 1) * 1024, :].reshape(8, 128, 8, 512)
    return np.ascontiguousarray(blk.transpose(2, 1, 0, 3)).reshape(8, 128, 4096)


def _layer_slabs(w_in, w_out, w_q, w_o, w_up, w_down):
    zin = _std_slabs(w_in)
    order = []
    for j in range(16):
        order += [j, 16 + j]
    for h in range(8):
        order += [32 + 2 * h, 32 + 2 * h + 1, 48 + 2 * h, 48 + 2 * h + 1]
    parts = [zin[order], _std_slabs(w_out), _std_slabs(w_q), _std_slabs(w_o)]
    up = _std_slabs(w_up)
    for g in range(16):
        parts.append(up[g * 8:(g + 1) * 8])
        parts.append(_down_slabs(w_down, g))
    out = np.concatenate(parts, axis=0)
    assert out.shape == (NS_L, 128, SLAB), out.shape
    return out


def _pcol(vec):
    return np.ascontiguousarray(vec.reshape(-1, 128).T)


def _pp(l, inp):
    cols = [_pcol(inp[k][l]) for k in ("g_pre_mix", "g_post_mix", "g_pre_xa", "g_post_xa", "g_pre_mlp", "g_post_mlp", "g_mem")]
    cw = inp["conv_w"][l]
    cwp = np.ascontiguousarray(cw.reshape(31, 16, 128).transpose(2, 1, 0)).reshape(128, 16 * 31)
    cols.append(cwp)
    cols += [_pcol(inp[k][l]) for k in ("conv_b", "ln_a_g", "ln_a_b", "ln_v_g", "ln_v_b")]
    out = np.concatenate(cols, axis=1).astype(np.float32)
    assert out.shape == (128, NPP), out.shape
    return out


def _consts():
    c = np.zeros((128, 384), np.float32)
    c[:, 0:128] = np.eye(128, dtype=np.float32)
    c[:, 128:256] = 1.0
    c[:, 256:384] = np.triu(np.ones((128, 128), np.float32))
    return c


_NC_CACHE = {}


def _get_nc(mode):
    if mode not in _NC_CACHE:
        _NC_CACHE[mode] = build(mode)
    return _NC_CACHE[mode]


def _prep_layer_inputs(l, inp):
    return dict(
        wl=_layer_slabs(inp["w_in"][l], inp["w_out"][l], inp["w_q"][l], inp["w_o"][l], inp["w_up"][l], inp["w_down"][l]),
        wkv=_std_slabs(inp["w_kv"][l]),
        pp=_pp(l, inp),
        wsp=np.ascontiguousarray(inp["w_spatial"][l].transpose(2, 0, 1)).reshape(128, 1024),
        bsb=np.ascontiguousarray(np.broadcast_to(inp["b_spatial"][l].reshape(1, 1024), (128, 1024))),
    )


def kernel_unfused(inp):
    x = inp["x"]
    mem = inp["mem"]
    cst = _consts()
    memTs = [np.ascontiguousarray(mem[b].T) for b in range(4)]
    nc = _get_nc("layer")
    for l in range(2):
        lw = _prep_layer_inputs(l, inp)
        in_maps = []
        for c in range(8):
            b, hf = c // 2, c % 2
            xs = x[b, hf * 1024:(hf + 1) * 1024, :]
            if hf == 0:
                xh = np.zeros((D, HALO), np.float32)
            else:
                xh = np.ascontiguousarray(x[b, 1024 - HALO:1024, :].T)
            m = dict(xT=np.ascontiguousarray(xs.T), xh=xh, memT=memTs[b], cst=cst,
                     flag=np.full((128, 1), float(hf), np.float32))
            m.update(lw)
            in_maps.append(m)
        res = run_bass_kernel_spmd(nc, in_maps, core_ids=list(range(8)))
        xn = np.empty_like(x)
        for c in range(8):
            b, hf = c // 2, c % 2
            xn[b, hf * 1024:(hf + 1) * 1024, :] = res.results[c]["outT"].T
        x = xn
    return x


def kernel_fused(inp):
    x = inp["x"]
    mem = inp["mem"]
    cst = _consts()
    l0 = _prep_layer_inputs(0, inp)
    l1 = _prep_layer_inputs(1, inp)
    shared = {k: np.concatenate([l0[k], l1[k]], axis=0) for k in l0}
    nc = _get_nc("fused")
    in_maps = []
    for c in range(8):
        b, hf = c // 2, c % 2
        xs = np.zeros((1152, D), np.float32)
        if hf == 1:
            xs[0:128] = x[b, 896:1024, :]
        xs[128:] = x[b, hf * 1024:(hf + 1) * 1024, :]
        m = dict(xT=np.ascontiguousarray(xs.T), memT=np.ascontiguousarray(mem[b].T), cst=cst,
                 flag=np.full((128, 1), float(hf), np.float32))
        m.update(shared)
        in_maps.append(m)
    res = run_bass_kernel_spmd(nc, in_maps, core_ids=list(range(8)))
    out = np.empty_like(x)
    for c in range(8):
        b, hf = c // 2, c % 2
        out[b, hf * 1024:(hf + 1) * 1024, :] = res.results[c]["outT"].T
    return out


MODE = "fused"


def kernel(**inputs):
    inp = {k: np.asarray(v) for k, v in inputs.items()}
    if MODE == "fused":
        return kernel_fused(inp).astype(np.float32)
    return kernel_unfused(inp).astype(np.float32)
```

```python
import numpy as np
from contextlib import ExitStack
import concourse.bass as bass
import concourse.mybir as mybir
from concourse.bass_utils import run_bass_kernel_spmd

F32 = mybir.dt.float32
BF16 = mybir.dt.bfloat16
AF = mybir.ActivationFunctionType
ALU = mybir.AluOpType
AX = mybir.AxisListType

D = 4096
NCH = 32
EPS = 1e-6
CELL = 256
NSLOT = 3
SLAB = 4096
NS_L = 416
NS_KV = 64
HALO = 32
GELU_TANH = True
USE_POW = False

SB_BASE = 16640
X_OFF = SB_BASE
ACC_OFF = SB_BASE + 65536
H_OFF = SB_BASE + 131072
W_OFF = SB_BASE + 163840
HID_OFF = W_OFF + NSLOT * 8192
MISC_OFF = HID_OFF + 8192
C_OFF = MISC_OFF
PP_OFF = C_OFF + 768
SQ_OFF = PP_OFF + 3328
HS_OFF = SQ_OFF + 2048
XH_OFF = HS_OFF + 2048
HH_OFF = XH_OFF + 4096
DG_OFF = HH_OFF + 2048
SM_OFF = DG_OFF + 1024

PC_GPRE_MIX, PC_GPOST_MIX, PC_GPRE_XA, PC_GPOST_XA, PC_GPRE_MLP, PC_GPOST_MLP, PC_GMEM = 0, 32, 64, 96, 128, 160, 192
PC_CONVW, PC_CONVB, PC_LNAG, PC_LNAB, PC_LNVG, PC_LNVB = 224, 720, 736, 752, 768, 784
NPP = 800


class V:
    __slots__ = ("ap", "keys", "excl")

    def __init__(self, ap, keys, excl=False):
        self.ap = ap
        self.keys = keys
        self.excl = excl


def _cells(space, lo, hi):
    return [(space, i) for i in range(lo // CELL, (hi - 1) // CELL + 1)]


class MT:
    def __init__(self, h, fshape, es, space, off):
        self.h = h
        self.fshape = list(fshape)
        self.es = es
        self.space = space
        self.off = off
        st = [1] * len(fshape)
        for i in range(len(fshape) - 2, -1, -1):
            st[i] = st[i + 1] * fshape[i + 1]
        self.st = st

    def __call__(self, *idx, p=None):
        key = [slice(None) if p is None else slice(p[0], p[1])]
        lo = 0
        hi = 0
        for d, n in enumerate(self.fshape):
            i = idx[d] if d < len(idx) else None
            if i is None:
                a, b = 0, n
                key.append(slice(None))
            elif isinstance(i, int):
                a, b = i, i + 1
                key.append(i)
            else:
                a, b = i
                key.append(slice(a, b))
            lo += a * self.st[d]
            hi += (b - 1) * self.st[d]
        hi += 1
        if self.space.startswith("ps"):
            return V(self.h[tuple(key)], [(self.space, 0)], True)
        return V(self.h[tuple(key)], _cells(self.space, self.off + lo * self.es, self.off + hi * self.es))


class Prog:
    def __init__(self):
        self.recs = []
        self.cells = {}
        self.last_dma = {}

    def add(self, eng, fn, reads=(), writes=(), dma=None):
        i = len(self.recs)
        raw = set()
        oth = set()
        cells = self.cells
        for v in reads:
            for k in v.keys:
                c = cells.get(k)
                if c is not None and c[0] is not None:
                    raw.add(c[0])
                if c is not None and v.excl:
                    for rk2, j2 in c[1].items():
                        if rk2 != eng:
                            oth.add(j2)
        for v in writes:
            for k in v.keys:
                c = cells.get(k)
                if c is not None:
                    if c[0] is not None:
                        oth.add(c[0])
                    oth.update(c[1].values())
        if dma is not None and dma in self.last_dma:
            oth.add(self.last_dma[dma])
        rk = eng if dma is None else "dma:" + dma
        for v in reads:
            for k in v.keys:
                c = cells.get(k)
                if c is None:
                    c = [None, {}]
                    cells[k] = c
                c[1][rk] = i
        for v in writes:
            for k in v.keys:
                cells[k] = [i, {}]
        if dma is not None:
            self.last_dma[dma] = i
        self.recs.append((eng, fn, raw, oth - raw, dma))
        return i

    def finalize(self):
        recs = self.recs
        n = len(recs)
        red = [None] * n
        need = set()
        for i, (eng, fn, raw, oth, dma) in enumerate(recs):
            best = {}
            for grp, israw in ((raw, True), (oth, False)):
                for j in grp:
                    je, _, _, _, jd = recs[j]
                    if jd is None and dma is None and je == eng:
                        if eng == "pe" or not israw:
                            continue
                    key = ("E", je) if jd is None else ("D", jd)
                    if key not in best or best[key] < j:
                        best[key] = j
            red[i] = best
            need.update(best.values())
        sig = [None] * n
        cnt = {}
        for i, (eng, fn, raw, oth, dma) in enumerate(recs):
            if dma is not None:
                k = ("D", dma)
                cnt[k] = cnt.get(k, 0) + 16
                sig[i] = (k, cnt[k], 16)
            elif i in need:
                k = ("E", eng)
                cnt[k] = cnt.get(k, 0) + 1
                sig[i] = (k, cnt[k], 1)
        self.red = red
        self.sig = sig
        self.semkeys = sorted(cnt.keys())
        self.final = dict(cnt)

    def emit(self, nc, es, final_wait_keys):
        sems = {}
        for k in self.semkeys:
            sems[k] = es.enter_context(nc.semaphore("s_%s_%s" % (k[0], k[1].replace(":", "_"))))
        block = es.enter_context(nc.Block())
        recs, red, sig = self.recs, self.red, self.sig
        per = {"pe": [], "act": [], "dve": [], "pool": [], "sp": []}
        for i, r in enumerate(recs):
            per[r[0]].append(i)

        def run(engname, e):
            waited = {}
            for i in per[engname]:
                for k, j in red[i].items():
                    sk, val, _ = sig[j]
                    if waited.get(sk, 0) < val:
                        e.wait_ge(sems[sk], val)
                        waited[sk] = val
                ins = recs[i][1](e)
                s = sig[i]
                if s is not None:
                    ins.then_inc(sems[s[0]], s[2])
            if engname == "sp":
                for k in final_wait_keys:
                    if k in self.final:
                        e.wait_ge(sems[k], self.final[k])

        @block.tensor
        def _(e):
            run("pe", e)

        @block.scalar
        def _(e):
            run("act", e)

        @block.vector
        def _(e):
            run("dve", e)

        @block.gpsimd
        def _(e):
            run("pool", e)

        @block.sync
        def _(e):
            run("sp", e)


class K:
    pass


def build(mode):
    fused = mode == "fused"
    NL = 2 if fused else 1
    NTOK = 1152 if fused else 1024
    nc = bass.Bass("TRN2", target_bir_lowering=False)
    P = Prog()
    es = ExitStack()

    def din(name, shape, dt=F32):
        return nc.dram_tensor(name, list(shape), dt, kind="ExternalInput").ap()

    xT = din("xT", [D, NTOK])
    xh_d = None if fused else din("xh", [D, HALO])
    memT = din("memT", [D, 256])
    wl_d = din("wl", [NL * NS_L, 128, SLAB])
    wkv_d = din("wkv", [NL * NS_KV, 128, SLAB])
    pp_d = din("pp", [NL * 128, NPP])
    wsp_d = din("wsp", [NL * 128, 1024])
    bsb_d = din("bsb", [NL * 128, 1024])
    cst_d = din("cst", [128, 384])
    flag_d = din("flag", [128, 1])
    outT = nc.dram_tensor("outT", [D, 1024], F32, kind="ExternalOutput").ap()
    Kd = [nc.dram_tensor("Kd%d" % l, [128, 32, 256], BF16).ap() for l in range(NL)]
    Vd = [nc.dram_tensor("Vd%d" % l, [128, 2, 4096], BF16).ap() for l in range(NL)]

    tcache = {}

    def sb(name, fshape, dt, off):
        key = (name, tuple(fshape), off)
        if key not in tcache:
            h = nc.alloc_sbuf_tensor_at("%s_%d" % (name, len(tcache)), [128] + list(fshape), dt, offset=off)
            tcache[key] = MT(h, fshape, 2 if dt == BF16 else 4, "sb", off)
        return tcache[key]

    PSB = []
    for b in range(7):
        h = es.enter_context(nc.psum_tensor("psb%d" % b, [128, 512], F32))
        PSB.append(MT(h, [512], 4, "ps%d" % b, 0))
    h7 = es.enter_context(nc.psum_tensor("psb7", [128, 1024], BF16))
    PS7 = MT(h7, [1024], 2, "ps7", 0)
    rot = [0]

    def nextbank():
        b = PSB[(0, 1, 2, 3, 6)[rot[0] % 5]]
        rot[0] += 1
        return b

    CST = sb("cst", [3, 128], BF16, C_OFF)
    ident = CST(0)
    ones = CST(1)
    tril = CST(2)
    PPt = sb("pp", [NPP], F32, PP_OFF)
    SQt = sb("sqt", [2, 512], BF16, SQ_OFF)
    RLt = sb("rlt", [512], F32, SQ_OFF)
    HSt = sb("hsave", [2, 16, HALO], BF16, HS_OFF)
    XH1 = sb("xh1", [32, HALO], F32, XH_OFF)
    HHt = sb("hhalo", [32, HALO], BF16, HH_OFF)
    DGt = sb("diag", [16, 128], BF16, HID_OFF)
    SMt = sb("small", [8], F32, SM_OFF)
    WSL = [sb("wslot%d" % s, [SLAB], BF16, W_OFF + s * 8192) for s in range(NSLOT)]

    def dkey(name):
        return [V(None, [("dr", name)])]

    def mm(out, lhsT, rhs, start, stop):
        P.add("pe", lambda e, o=out.ap, l=lhsT.ap, r=rhs.ap, s=start, t=stop: e.matmul(o, l, r, start=s, stop=t),
              reads=[lhsT, rhs], writes=[out])

    def tr(out, in_, idv):
        P.add("pe", lambda e, o=out.ap, i=in_.ap, d=idv.ap: e.transpose(o, i, d), reads=[in_, idv], writes=[out])

    def act(out, in_, func, bias=None, scale=None, accum=None, eng="act"):
        rd = [in_]
        kw = {}
        if bias is not None:
            if isinstance(bias, V):
                rd.append(bias)
                kw["bias"] = bias.ap
            else:
                kw["bias"] = bias
        if scale is not None:
            if isinstance(scale, V):
                rd.append(scale)
                kw["scale"] = scale.ap
            else:
                kw["scale"] = scale
        wr = [out]
        if accum is not None:
            wr.append(accum)
            kw["accum_out"] = accum.ap
        P.add("act", lambda e, o=out.ap, i=in_.ap, f=func, kw=kw: e.activation(out=o, in_=i, func=f, **kw),
              reads=rd, writes=wr)

    def stt(out, in0, scalar, in1, op0, op1, eng="dve"):
        rd = [in0, in1]
        if isinstance(scalar, V):
            rd.append(scalar)
            sc = scalar.ap
        else:
            sc = scalar
        P.add(eng, lambda e, o=out.ap, a=in0.ap, s=sc, b=in1.ap, p0=op0, p1=op1:
              e.scalar_tensor_tensor(out=o, in0=a, scalar=s, in1=b, op0=p0, op1=p1), reads=rd, writes=[out])

    def tt(out, in0, in1, op, eng="dve"):
        P.add(eng, lambda e, o=out.ap, a=in0.ap, b=in1.ap, p=op: e.tensor_tensor(out=o, in0=a, in1=b, op=p),
              reads=[in0, in1], writes=[out])

    def ts(out, in0, s1, s2, op0, op1=None, eng="dve"):
        rd = [in0]
        a1 = s1
        a2 = s2
        if isinstance(s1, V):
            rd.append(s1)
            a1 = s1.ap
        if isinstance(s2, V):
            rd.append(s2)
            a2 = s2.ap
        if op1 is None:
            P.add(eng, lambda e, o=out.ap, a=in0.ap, x=a1, p0=op0:
                  e.tensor_scalar(out=o, in0=a, scalar1=x, scalar2=None, op0=p0), reads=rd, writes=[out])
        else:
            P.add(eng, lambda e, o=out.ap, a=in0.ap, x=a1, y=a2, p0=op0, p1=op1:
                  e.tensor_scalar(out=o, in0=a, scalar1=x, scalar2=y, op0=p0, op1=p1), reads=rd, writes=[out])

    def cp(out, in_, eng="dve"):
        P.add(eng, lambda e, o=out.ap, i=in_.ap: e.tensor_copy(out=o, in_=i), reads=[in_], writes=[out])

    def dma(q, key, out_ap, in_ap, reads, writes):
        P.add(q, lambda e, o=out_ap, i=in_ap: e.dma_start(out=o, in_=i), reads=reads, writes=writes, dma=key)

    def rstd_inplace(v, inv_n):
        ts(v, v, inv_n, EPS, ALU.mult, ALU.add)
        if USE_POW:
            ts(v, v, -0.5, None, ALU.pow)
        else:
            act(v, v, AF.Ln)
            act(v, v, AF.Exp, scale=-0.5)

    def ppc(col, n=1):
        return PPt((col, col + n))

    ESL = [sb("eslot%d" % s_, [SLAB], BF16, X_OFF + 32768 + s_ * 8192) for s_ in range(4)]
    ESL += [sb("eslot%d" % (4 + s_), [SLAB], BF16, H_OFF + 16384 + s_ * 8192) for s_ in range(2)]
    rings = {0: WSL, 1: ESL}
    ring_members = {0: [], 1: []}
    wseq = []
    ord_in_ring = []
    wstate = {"issued": 0, "used": 0}
    MAXLA = 9

    def plan_slabs(lst, ring=0):
        for ap_ in lst:
            k = len(wseq)
            wseq.append((ap_, ring))
            ord_in_ring.append(len(ring_members[ring]))
            ring_members[ring].append(k)

    def slot_of(k):
        r = wseq[k][1]
        return rings[r][ord_in_ring[k] % len(rings[r])], "w%d_%d" % (r, ord_in_ring[k] % len(rings[r]))

    def can_issue(k, i):
        r = wseq[k][1]
        n = ord_in_ring[k]
        R = len(rings[r])
        return n < R or ring_members[r][n - R] < i

    def pump(i):
        while wstate["issued"] < len(wseq) and wstate["issued"] < i + MAXLA and can_issue(wstate["issued"], i):
            k = wstate["issued"]
            slot, key = slot_of(k)
            v = slot()
            dma("pool", key, v.ap, wseq[k][0], [], [v])
            wstate["issued"] += 1

    def next_slab():
        i = wstate["used"]
        pump(i)
        assert wstate["issued"] > i
        wstate["used"] += 1
        return slot_of(i)[0]

    def prenorm(T, Xv, gcol, Hv, nch=NCH):
        stat = PSB[4]((0, T))
        for c in range(nch):
            sq = SQt(c % 2, (0, T))
            act(sq, Xv(c), AF.Square)
            mm(stat, ones, sq, c == 0, c == nch - 1)
        rstd_inplace(stat, 1.0 / D)
        for c in range(nch):
            stt(Hv(c), Xv(c), ppc(gcol + c), stat, ALU.mult, ALU.mult)

    def postnorm_residual(T, ACCv, gcol, Xv, stat):
        rstd_inplace(stat, 1.0 / D)
        for c in range(NCH):
            a = ACCv(c)
            stt(a, a, ppc(gcol + c), stat, ALU.mult, ALU.mult)
            tt(Xv(c), Xv(c), a, ALU.add)

    def group_ln(T, srcs, tmpC, tmpB, gcol, bcol, func, dsts, post=None):
        s1 = PSB[4]((0, T))
        s2 = PSB[5]((0, T))
        for i in range(2):
            cp(tmpC(i), srcs[i])
            act(SQt(i, (0, T)), srcs[i], AF.Square)
        yield
        for i in range(2):
            mm(s1, ones, tmpC(i), i == 0, i == 1)
        for i in range(2):
            mm(s2, ones, SQt(i, (0, T)), i == 0, i == 1)
        ts(s1, s1, 1.0 / 256.0, None, ALU.mult)
        act(tmpB, s1, AF.Square)
        stt(s2, s2, 1.0 / 256.0, tmpB, ALU.mult, ALU.subtract)
        ts(s2, s2, EPS, None, ALU.add)
        if USE_POW:
            ts(s2, s2, -0.5, None, ALU.pow)
        else:
            act(s2, s2, AF.Ln)
            act(s2, s2, AF.Exp, scale=-0.5)
        yield
        for i in range(2):
            tt(srcs[i], srcs[i], s1, ALU.subtract)
            tt(srcs[i], srcs[i], s2, ALU.mult)
            act(dsts[i], srcs[i], func, bias=ppc(bcol + i), scale=ppc(gcol + i))
        yield
        if post is not None:
            post()
        yield

    def step(g):
        if g is not None:
            next(g, None)

    def gelu_from_psum(dst, src, tmp):
        if GELU_TANH:
            act(dst, src, AF.Gelu_apprx_tanh)
        else:
            act(tmp, src, AF.Square)
            ts(tmp, tmp, 0.044715, 1.0, ALU.mult, ALU.add)
            tt(tmp, tmp, src, ALU.mult)
            act(tmp, tmp, AF.Sigmoid, scale=1.5957691216057308)
            tt(dst, tmp, src, ALU.mult)

    def load_layer_params(l):
        v = PPt()
        dma("sp", "pp", v.ap, pp_d[l * 128:(l + 1) * 128, :], [], [v])

    def views(T):
        o = K()
        o.X = sb("X", [NCH, T], F32, X_OFF)
        o.ACC = sb("ACC", [NCH, T], F32, ACC_OFF)
        o.H = sb("H", [NCH, T], BF16, H_OFF)
        a = ACC_OFF
        o.hh = sb("hh", [16, T + HALO], BF16, a)
        a += 16 * (512 + HALO) * 2
        o.u = sb("u", [16, T], BF16, a)
        a += 16 * 512 * 2
        o.vnT = sb("vnT", [T // 128, 2048], BF16, a)
        a += 4 * 2048 * 2
        o.bsb = sb("bsbt", [8, 128], F32, a)
        a += 4096
        o.Wt = sb("Wt", [8, 128], BF16, a)
        a += 2048
        o.tmpA = sb("tmpA", [2, T], F32, a)
        o.tmpA2 = sb("tmpA2", [2, T], F32, HID_OFF + 4096)
        o.xht = sb("xht", [32, HALO], F32, a)
        a += 4096
        o.tmpB = sb("tmpB", [T], F32, a)
        o.sgh = sb("sgh", [HALO], F32, a)
        a += 2048
        o.tmpC = sb("tmpC", [2, T], BF16, a)
        a += 2048
        assert a <= ACC_OFF + 65536
        o.q = sb("q", [NCH, T], BF16, ACC_OFF)
        o.Kt = sb("Kt", [NCH, 256], BF16, ACC_OFF + 32768)
        o.Vt = sb("Vt", [2, 4096], BF16, ACC_OFF + 49152)
        o.hid = sb("hid", [8, T], BF16, HID_OFF)
        o.Ex = sb("Ex", [256], F32, HID_OFF)
        o.Pb = sb("Pb", [256], BF16, HID_OFF + 1024)
        o.PTs = sb("PTs", [2, 128], BF16, HID_OFF + 1536)
        return o

    def mix(T, o, l, halo_mode, xh_src=None, save_halo=True):
        X, H, ACC = o.X, o.H, o.ACC
        NB = T // 128
        v = o.bsb()
        dma("sp", "bsb", v.ap, bsb_d[l * 128:(l + 1) * 128, :], [], [v])
        v = o.Wt()
        dma("pool", "wsp", v.ap, wsp_d[l * 128:(l + 1) * 128, :], [], [v])
        for h in range(8):
            tt(o.Wt(h), o.Wt(h), tril, ALU.mult)
        prenorm(T, X, PC_GPRE_MIX, H)
        if halo_mode == "x":
            stat = PSB[5]((0, HALO))
            for c in range(NCH):
                sq = SQt(c % 2, (0, HALO))
                act(sq, xh_src(c), AF.Square)
                mm(stat, ones, sq, c == 0, c == NCH - 1)
            rstd_inplace(stat, 1.0 / D)
            for c in range(NCH):
                stt(HHt(c), xh_src(c), ppc(PC_GPRE_MIX + c), stat, ALU.mult, ALU.mult)
        for j in range(16):
            w = next_slab()
            ba = nextbank()
            bha = PSB[5]((64, 64 + HALO))
            bhg = PSB[5]((128, 128 + HALO))
            for kc in range(NCH):
                mm(ba((0, T)), w((kc * 128, kc * 128 + 128)), H(kc), kc == 0, kc == NCH - 1)
                if halo_mode == "x":
                    mm(bha, w((kc * 128, kc * 128 + 128)), HHt(kc), kc == 0, kc == NCH - 1)
            w = next_slab()
            bg = nextbank()
            for kc in range(NCH):
                mm(bg((0, T)), w((kc * 128, kc * 128 + 128)), H(kc), kc == 0, kc == NCH - 1)
                if halo_mode == "x":
                    mm(bhg, w((kc * 128, kc * 128 + 128)), HHt(kc), kc == 0, kc == NCH - 1)
            act(o.tmpB(), bg((0, T)), AF.Sigmoid)
            tt(o.hh(j, (HALO, HALO + T)), ba((0, T)), o.tmpB(), ALU.mult)
            if halo_mode == "x":
                act(o.sgh(), bhg, AF.Sigmoid)
                tt(o.hh(j, (0, HALO)), bha, o.sgh(), ALU.mult)
            elif halo_mode == "saved":
                cp(o.hh(j, (0, HALO)), HSt(l, j))
            else:
                P.add("dve", lambda e, a=o.hh(j, (0, HALO)).ap: e.memset(a, 0.0), reads=[], writes=[o.hh(j, (0, HALO))])
            if save_halo:
                cp(HSt(l, j), o.hh(j, (T, T + HALO)))
        pending = None
        for h in range(8):
            tA = o.tmpA2 if h % 2 else o.tmpA
            for kind, i in (("u", 0), ("u", 1), ("v", 0), ("v", 1)):
                step(pending)
                w = next_slab()
                b = nextbank()
                for kc in range(NCH):
                    mm(b((0, T)), w((kc * 128, kc * 128 + 128)), H(kc), kc == 0, kc == NCH - 1)
                if kind == "u":
                    gelu_from_psum(o.u(2 * h + i), b((0, T)), o.tmpB())
                else:
                    gelu_from_psum(tA(i), b((0, T)), o.tmpB())

            def post(h=h):
                for i in range(2):
                    for tb in range(NB):
                        pt = PS7(((i * NB + tb) * 128, (i * NB + tb) * 128 + 128))
                        tr(pt, o.tmpC(i, (tb * 128, tb * 128 + 128)), ident)
                        cp(o.vnT(tb, (h * 256 + i * 128, h * 256 + i * 128 + 128)), pt)

            pending = group_ln(T, [tA(0), tA(1)], o.tmpC, o.tmpB(), PC_LNVG + 2 * h, PC_LNVB + 2 * h, AF.Identity,
                               [o.tmpC(0), o.tmpC(1)], post)
        dg = [0]
        for j in range(8):
            tA = o.tmpA2 if j % 2 else o.tmpA
            for i in range(2):
                step(pending)
                c = 2 * j + i
                b = nextbank()
                for k in range(31):
                    d = DGt(dg[0] % 16)
                    dg[0] += 1
                    if k % 2 == 0:
                        ts(d, ident, ppc(PC_CONVW + c * 31 + k), None, ALU.mult)
                    else:
                        act(d, ident, AF.Identity, scale=ppc(PC_CONVW + c * 31 + k))
                    mm(b((0, T)), d, o.hh(c, (k + 2, k + 2 + T)), k == 0, k == 30)
                act(tA(i), b((0, T)), AF.Identity, bias=ppc(PC_CONVB + c))
            step(pending)
            step(pending)
            pending = group_ln(T, [tA(0), tA(1)], o.tmpC, o.tmpB(), PC_LNAG + 2 * j, PC_LNAB + 2 * j, AF.Silu,
                               [H(2 * j), H(2 * j + 1)])
        for _ in range(4):
            step(pending)
        for h in range(8):
            for i in range(2):
                b = nextbank()
                for tb in range(NB):
                    mm(b((tb * 128, tb * 128 + 128)), o.vnT(tb, (h * 256 + i * 128, h * 256 + i * 128 + 128)), o.Wt(h),
                       True, True)
                for tb in range(NB):
                    tt(o.tmpB((tb * 128, tb * 128 + 128)), b((tb * 128, tb * 128 + 128)), o.bsb(h), ALU.add)
                tt(H(16 + 2 * h + i), o.tmpB(), o.u(2 * h + i), ALU.mult)
        proj_out(T, o, PC_GPOST_MIX)

    def proj_out(T, o, gcol):
        stat = PSB[5]((0, T))
        pend = None
        for n in range(NCH):
            w = next_slab()
            b = nextbank()
            for kc in range(NCH):
                mm(b((0, T)), w((kc * 128, kc * 128 + 128)), o.H(kc), kc == 0, kc == NCH - 1)
            if pend is not None:
                mm(stat, ones, pend[0], pend[1] == 0, False)
            act(o.ACC(n), b((0, T)), AF.Copy)
            sq = SQt(n % 2, (0, T))
            act(sq, b((0, T)), AF.Square)
            pend = (sq, n)
        mm(stat, ones, pend[0], False, True)
        postnorm_residual(T, o.ACC, gcol, o.X, stat)

    def xattn(T, o, l):
        X, H = o.X, o.H
        NB = T // 128
        prenorm(T, X, PC_GPRE_XA, H)
        v = o.Kt()
        dma("sp", "kld", v.ap, Kd[l], dkey("K%d" % l), [v])
        v = o.Vt()
        dma("sp", "vld", v.ap, Vd[l], dkey("V%d" % l), [v])
        for n in range(NCH):
            w = next_slab()
            b = nextbank()
            for kc in range(NCH):
                mm(b((0, T)), w((kc * 128, kc * 128 + 128)), H(kc), kc == 0, kc == NCH - 1)
            act(o.q(n), b((0, T)), AF.Copy)
        pairs = [(tb, hd) for tb in range(NB) for hd in range(4)]
        NP = len(pairs)
        Exb = [sb("Exb%d" % d_, [256], F32, HID_OFF + d_ * 1024) for d_ in range(2)]
        Pbb = [sb("Pbb%d" % d_, [256], BF16, HID_OFF + 2048 + d_ * 512) for d_ in range(2)]
        PTb = [sb("PTb%d" % d_, [2, 128], BF16, HID_OFF + 3072 + d_ * 512) for d_ in range(2)]
        STb = [[sb("stb%d_%d" % (d_, k_), [1], F32, HID_OFF + 4096 + (d_ * 4 + k_) * 256) for k_ in range(4)]
               for d_ in range(2)]

        def scores(p):
            tb, hd = pairs[p]
            S = nextbank()((0, 256))
            for c in range(8):
                mm(S, o.q(hd * 8 + c, (tb * 128, tb * 128 + 128)), o.Kt(hd * 8 + c), c == 0, c == 7)
            return S

        def softmax(p, S):
            d_ = p % 2
            mx, nmx, sm, rsm = [t_() for t_ in STb[d_]]
            P.add("dve", lambda e, o_=mx.ap, i_=S.ap: e.reduce_max(out=o_, in_=i_, axis=AX.X), reads=[S], writes=[mx])
            ts(nmx, mx, -1.0 / 32.0, None, ALU.mult)
            act(Exb[d_](), S, AF.Exp, bias=nmx, scale=1.0 / 32.0, accum=sm)
            P.add("dve", lambda e, o_=rsm.ap, i_=sm.ap: e.reciprocal(out=o_, in_=i_), reads=[sm], writes=[rsm])
            ts(Pbb[d_](), Exb[d_](), rsm, None, ALU.mult)

        def transposes(p):
            d_ = p % 2
            for mb in range(2):
                pt = PS7((d_ * 256 + mb * 128, d_ * 256 + mb * 128 + 128))
                tr(pt, Pbb[d_]((mb * 128, mb * 128 + 128)), ident)
                cp(PTb[d_](mb), pt)

        def pv(p):
            tb, hd = pairs[p]
            d_ = p % 2
            for half in range(2):
                bo = nextbank()
                for cc in range(4):
                    c = half * 4 + cc
                    for mb in range(2):
                        mm(bo((cc * 128, cc * 128 + 128)),
                           o.Vt(mb, (hd * 1024 + c * 128, hd * 1024 + c * 128 + 128)), PTb[d_](mb), mb == 0, mb == 1)
                for cc in range(4):
                    c = half * 4 + cc
                    act(H(hd * 8 + c, (tb * 128, tb * 128 + 128)), bo((cc * 128, cc * 128 + 128)), AF.Copy)

        Sv = {}
        for p in range(min(2, NP)):
            Sv[p] = scores(p)
            softmax(p, Sv[p])
        for p in range(NP):
            if p + 2 < NP:
                Sv[p + 2] = scores(p + 2)
            transposes(p)
            if p + 2 < NP:
                softmax(p + 2, Sv[p + 2])
            pv(p)
        proj_out(T, o, PC_GPOST_XA)

    def mlp(T, o):
        X, H, ACC = o.X, o.H, o.ACC
        prenorm(T, X, PC_GPRE_MLP, H)
        for g in range(16):
            for j in range(8):
                w = next_slab()
                b = nextbank()
                for kc in range(NCH):
                    mm(b((0, T)), w((kc * 128, kc * 128 + 128)), H(kc), kc == 0, kc == NCH - 1)
                act(RLt((0, T)), b((0, T)), AF.Relu)
                tt(o.hid(j), RLt((0, T)), b((0, T)), ALU.mult)
            for qd in range(8):
                w = next_slab()
                for r in range(4):
                    n = qd * 4 + r
                    b = nextbank()
                    for kc in range(8):
                        mm(b((0, T)), w((kc * 512 + r * 128, kc * 512 + r * 128 + 128)), o.hid(kc), kc == 0, kc == 7)
                    if g == 0:
                        act(ACC(n), b((0, T)), AF.Copy)
                    else:
                        tt(ACC(n), ACC(n), b((0, T)), ALU.add)
        stat = PSB[5]((0, T))
        for n in range(NCH):
            sq = SQt(n % 2, (0, T))
            act(sq, ACC(n), AF.Square)
            mm(stat, ones, sq, n == 0, n == NCH - 1)
        postnorm_residual(T, ACC, PC_GPOST_MLP, X, stat)

    def kv_phase(l):
        T = 256
        Xm = sb("Xm", [NCH, T], F32, X_OFF)
        Hm = sb("Hm", [NCH, T], BF16, H_OFF)
        Kst = sb("Kst", [NCH, 256], BF16, ACC_OFF)
        Vst = sb("Vst", [2, 4096], BF16, ACC_OFF + 16384)
        vtmp = sb("vtmp", [256], BF16, HID_OFF)
        for q4 in range(4):
            v = Xm((q4 * 8, q4 * 8 + 8))
            src = memT[q4 * 1024:(q4 + 1) * 1024, :].rearrange("(c p) t -> p c t", p=128)
            dma("sp", "xin%d" % q4, v.ap, src, [], [v])
        prenorm(T, Xm, PC_GMEM, Hm)
        for n in range(64):
            w = next_slab()
            b = nextbank()
            for kc in range(NCH):
                mm(b((0, T)), w((kc * 128, kc * 128 + 128)), Hm(kc), kc == 0, kc == NCH - 1)
            if n < 32:
                act(Kst(n), b((0, T)), AF.Copy)
            else:
                act(vtmp(), b((0, T)), AF.Copy)
                for mb in range(2):
                    pt = PS7((mb * 128, mb * 128 + 128))
                    tr(pt, vtmp((mb * 128, mb * 128 + 128)), ident)
                    cp(Vst(mb, ((n - 32) * 128, (n - 32) * 128 + 128)), pt)
        v = Kst()
        dma("sp", "kst", Kd[l], v.ap, [v], dkey("K%d" % l))
        v = Vst()
        dma("sp", "vst", Vd[l], v.ap, [v], dkey("V%d" % l))

    def load_x(T, o, col0):
        for q4 in range(4):
            v = o.X((q4 * 8, q4 * 8 + 8))
            src = xT[q4 * 1024:(q4 + 1) * 1024, col0:col0 + T].rearrange("(c p) t -> p c t", p=128)
            dma("sp", "xin%d" % q4, v.ap, src, [], [v])

    def store_x(T, o, col0):
        for q4 in range(4):
            v = o.X((q4 * 8, q4 * 8 + 8))
            dst = outT[q4 * 1024:(q4 + 1) * 1024, col0:col0 + T].rearrange("(c p) t -> p c t", p=128)
            dma("sp", "xout%d" % q4, dst, v.ap, [v], dkey("out%d_%d" % (col0, q4)))

    def layer_slabs(l):
        return [wl_d[l * NS_L + s] for s in range(NS_L)]

    def kv_slabs(l):
        return [wkv_d[l * NS_KV + s] for s in range(NS_KV)]

    if fused:
        plan_slabs(kv_slabs(0) + kv_slabs(1) + layer_slabs(0), 1)
        plan_slabs(layer_slabs(0) + layer_slabs(1) + layer_slabs(0) + layer_slabs(1), 0)
    else:
        plan_slabs(kv_slabs(0), 1)
        plan_slabs(layer_slabs(0) + layer_slabs(0), 0)

    v = CST()
    dma("pool", "cst", v.ap, cst_d, [], [v])
    flg = SMt((4, 5))
    dma("sp", "flag", flg.ap, flag_d, [], [flg])

    def layer(T, o, l, halo_mode, xh_src=None, save_halo=True):
        load_layer_params(l)
        mix(T, o, l, halo_mode, xh_src, save_halo)
        xattn(T, o, l)
        mlp(T, o)

    o512 = views(512)
    if fused:
        load_layer_params(0)
        kv_phase(0)
        load_layer_params(1)
        kv_phase(1)
        oE = views(128)
        load_x(128, oE, 0)
        layer(128, oE, 0, "zero")
        for c in range(NCH):
            ts(XH1(c), oE.X(c, (128 - HALO, 128)), flg, None, ALU.mult)
        load_x(512, o512, 128)
        layer(512, o512, 0, "saved")
        layer(512, o512, 1, "x", xh_src=XH1)
        store_x(512, o512, 0)
        load_x(512, o512, 640)
        layer(512, o512, 0, "saved")
        layer(512, o512, 1, "saved")
        store_x(512, o512, 512)
    else:
        load_layer_params(0)
        kv_phase(0)
        load_x(512, o512, 0)
        for q4 in range(4):
            v = o512.xht((q4 * 8, q4 * 8 + 8))
            src = xh_d[q4 * 1024:(q4 + 1) * 1024, :].rearrange("(c p) t -> p c t", p=128)
            dma("sp", "xh%d" % q4, v.ap, src, [], [v])
        layer(512, o512, 0, "x", xh_src=o512.xht)
        store_x(512, o512, 0)
        load_x(512, o512, 512)
        layer(512, o512, 0, "saved")
        store_x(512, o512, 512)

    assert wstate["used"] == len(wseq), (wstate, len(wseq))
    P.finalize()
    outkeys = [k for k in P.semkeys if k[0] == "D" and k[1].startswith("xout")]
    with es:
        P.emit(nc, es, outkeys)
    return nc


def _std_slabs(W):
    Kd_, N = W.shape
    Wr = W.reshape(Kd_ // 128, 128, N // 128, 128)
    return np.ascontiguousarray(Wr.transpose(2, 1, 0, 3)).reshape(N // 128, 128, (Kd_ // 128) * 128)


def _down_slabs(Wd, g):
    blk = Wd[g * 1024:(g + 1) * 1024, :].reshape(8, 128, 8, 512)
    return np.ascontiguousarray(blk.transpose(2, 1, 0, 3)).reshape(8, 128, 4096)


def _layer_slabs(w_in, w_out, w_q, w_o, w_up, w_down):
    zin = _std_slabs(w_in)
    order = []
    for j in range(16):
        order += [j, 16 + j]
    for h in range(8):
        order += [32 + 2 * h, 32 + 2 * h + 1, 48 + 2 * h, 48 + 2 * h + 1]
    parts = [zin[order], _std_slabs(w_out), _std_slabs(w_q), _std_slabs(w_o)]
    up = _std_slabs(w_up)
    for g in range(16):
        parts.append(up[g * 8:(g + 1) * 8])
        parts.append(_down_slabs(w_down, g))
    out = np.concatenate(parts, axis=0)
    assert out.shape == (NS_L, 128, SLAB), out.shape
    return out


def _pcol(vec):
    return np.ascontiguousarray(vec.reshape(-1, 128).T)


def _pp(l, inp):
    cols = [_pcol(inp[k][l]) for k in ("g_pre_mix", "g_post_mix", "g_pre_xa", "g_post_xa", "g_pre_mlp", "g_post_mlp", "g_mem")]
    cw = inp["conv_w"][l]
    cwp = np.ascontiguousarray(cw.reshape(31, 16, 128).transpose(2, 1, 0)).reshape(128, 16 * 31)
    cols.append(cwp)
    cols += [_pcol(inp[k][l]) for k in ("conv_b", "ln_a_g", "ln_a_b", "ln_v_g", "ln_v_b")]
    out = np.concatenate(cols, axis=1).astype(np.float32)
    assert out.shape == (128, NPP), out.shape
    return out


def _consts():
    c = np.zeros((128, 384), np.float32)
    c[:, 0:128] = np.eye(128, dtype=np.float32)
    c[:, 128:256] = 1.0
    c[:, 256:384] = np.triu(np.ones((128, 128), np.float32))
    return c


_NC_CACHE = {}


def _get_nc(mode):
    if mode not in _NC_CACHE:
        _NC_CACHE[mode] = build(mode)
    return _NC_CACHE[mode]


def _prep_layer_inputs(l, inp):
    return dict(
        wl=_layer_slabs(inp["w_in"][l], inp["w_out"][l], inp["w_q"][l], inp["w_o"][l], inp["w_up"][l], inp["w_down"][l]),
        wkv=_std_slabs(inp["w_kv"][l]),
        pp=_pp(l, inp),
        wsp=np.ascontiguousarray(inp["w_spatial"][l].transpose(2, 0, 1)).reshape(128, 1024),
        bsb=np.ascontiguousarray(np.broadcast_to(inp["b_spatial"][l].reshape(1, 1024), (128, 1024))),
    )


def kernel_unfused(inp):
    x = inp["x"]
    mem = inp["mem"]
    cst = _consts()
    memTs = [np.ascontiguousarray(mem[b].T) for b in range(4)]
    nc = _get_nc("layer")
    for l in range(2):
        lw = _prep_layer_inputs(l, inp)
        in_maps = []
        for c in range(8):
            b, hf = c // 2, c % 2
            xs = x[b, hf * 1024:(hf + 1) * 1024, :]
            if hf == 0:
                xh = np.zeros((D, HALO), np.float32)
            else:
                xh = np.ascontiguousarray(x[b, 1024 - HALO:1024, :].T)
            m = dict(xT=np.ascontiguousarray(xs.T), xh=xh, memT=memTs[b], cst=cst,
                     flag=np.full((128, 1), float(hf), np.float32))
            m.update(lw)
            in_maps.append(m)
        res = run_bass_kernel_spmd(nc, in_maps, core_ids=list(range(8)))
        xn = np.empty_like(x)
        for c in range(8):
            b, hf = c // 2, c % 2
            xn[b, hf * 1024:(hf + 1) * 1024, :] = res.results[c]["outT"].T
        x = xn
    return x


def kernel_fused(inp):
    x = inp["x"]
    mem = inp["mem"]
    cst = _consts()
    l0 = _prep_layer_inputs(0, inp)
    l1 = _prep_layer_inputs(1, inp)
    shared = {k: np.concatenate([l0[k], l1[k]], axis=0) for k in l0}
    nc = _get_nc("fused")
    in_maps = []
    for c in range(8):
        b, hf = c // 2, c % 2
        xs = np.zeros((1152, D), np.float32)
        if hf == 1:
            xs[0:128] = x[b, 896:1024, :]
        xs[128:] = x[b, hf * 1024:(hf + 1) * 1024, :]
        m = dict(xT=np.ascontiguousarray(xs.T), memT=np.ascontiguousarray(mem[b].T), cst=cst,
                 flag=np.full((128, 1), float(hf), np.float32))
        m.update(shared)
        in_maps.append(m)
    res = run_bass_kernel_spmd(nc, in_maps, core_ids=list(range(8)))
    out = np.empty_like(x)
    for c in range(8):
        b, hf = c // 2, c % 2
        out[b, hf * 1024:(hf + 1) * 1024, :] = res.results[c]["outT"].T
    return out


MODE = "fused"


def kernel(**inputs):
    inp = {k: np.asarray(v) for k, v in inputs.items()}
    if MODE == "fused":
        return kernel_fused(inp).astype(np.float32)
    return kernel_unfused(inp).astype(np.float32)
```
